# Optimizing a Trainium2 kernel written in Bass

```python
import math
import jax, jax.numpy as jnp
from jax import lax
import numpy as np

D_MODEL = 2048
BATCH = 2
SEQ = 4096
DEPTH = 2

GRID_W = 64
CTX_LEN = 256
HEAD_DIM = 128
N_MOD = 9
FFN_HIDDEN = 5632
RMS_EPS = 1e-6
ROPE_THETA = 10000.0
Q_BLOCK = 128

N_HEADS_TOTAL = D_MODEL // HEAD_DIM
A_HEADS = (3 * N_HEADS_TOTAL) // 8
A_KV_HEADS = A_HEADS // 3
B_HEADS = N_HEADS_TOTAL // 4
C_HEADS = N_HEADS_TOTAL - A_HEADS - B_HEADS
B_SUB_DIM = HEAD_DIM // 2
A_WIDTH = A_HEADS * HEAD_DIM
A_KV_WIDTH = A_KV_HEADS * HEAD_DIM
B_WIDTH = B_HEADS * HEAD_DIM
C_WIDTH = C_HEADS * HEAD_DIM
MIX_WIDTH = A_WIDTH + B_WIDTH + C_WIDTH
IN_SIZES = (A_WIDTH, A_KV_WIDTH, A_KV_WIDTH, B_WIDTH, B_WIDTH, B_WIDTH, C_WIDTH, C_WIDTH, C_WIDTH)
IN_WIDTH = A_WIDTH + 2 * A_KV_WIDTH + 3 * B_WIDTH + 3 * C_WIDTH
NA_ROWS_MAX = 8
NA_COLS = 16

kernel_name = "hybrid_headgroup_diffusion_block"


def rms_norm(x, gain=None):
    xf = x.astype(jnp.float32)
    y = xf * lax.rsqrt(jnp.mean(xf * xf, axis=-1, keepdims=True) + RMS_EPS)
    if gain is not None:
        y = y * gain.astype(jnp.float32)
    return y.astype(x.dtype)


def modulate(xn, shift, scale):
    return xn * (1 + scale) + shift


def swiglu(h, w_gu, w_down):
    gate, up = jnp.split(h @ w_gu, 2, axis=-1)
    return (jax.nn.silu(gate) * up) @ w_down


def rope_1d(x, pos):
    half = x.shape[-1] // 2
    freqs = ROPE_THETA ** (-jnp.arange(half, dtype=jnp.float32) / half)
    ang = pos.astype(jnp.float32)[:, None] * freqs[None, :]
    cos = jnp.cos(ang)[None, :, None, :]
    sin = jnp.sin(ang)[None, :, None, :]
    xf = x.astype(jnp.float32)
    x1, x2 = xf[..., :half], xf[..., half:]
    return jnp.concatenate([x1 * cos - x2 * sin, x1 * sin + x2 * cos], axis=-1).astype(x.dtype)


def rope_2d(x, row, col):
    d = x.shape[-1] // 2
    return jnp.concatenate([rope_1d(x[..., :d], row), rope_1d(x[..., d:], col)], axis=-1)


def sweep_query_blocks(fn, q):
    b, s = q.shape[0], q.shape[1]
    nb = s // Q_BLOCK
    qb = jnp.moveaxis(q.reshape((b, nb, Q_BLOCK) + q.shape[2:]), 1, 0)
    out = lax.map(fn, qb)
    return jnp.moveaxis(out, 0, 1).reshape((b, s) + out.shape[3:])


def gqa_attend(q, k, v):
    b, nq, h, dh = q.shape
    hkv = k.shape[2]
    qg = q.reshape(b, nq, hkv, h // hkv, dh)
    s = jnp.einsum('bqkgd,blkd->bkgql', qg, k, preferred_element_type=jnp.float32) * (dh ** -0.5)
    p = jax.nn.softmax(s, axis=-1).astype(v.dtype)
    o = jnp.einsum('bkgql,blkd->bqkgd', p, v)
    return o.reshape(b, nq, h, dh)


def diff_attend(q, k, v, lam):
    ds = q.shape[-1]
    s = jnp.einsum('bqhmd,blhmd->bhmql', q, k, preferred_element_type=jnp.float32) * (ds ** -0.5)
    p = jax.nn.softmax(s, axis=-1)
    w = p[:, :, 0] - lam * p[:, :, 1]
    return jnp.einsum('bhql,blhd->bqhd', w.astype(v.dtype), v)


def neighborhood_attend(q, k, v, k_ctx, v_ctx, rel_bias):
    b, s, h, dh = q.shape
    rows = s // GRID_W
    wr = min(NA_ROWS_MAX, rows)
    n_win = wr * NA_COLS
    kg = k.reshape(b, rows, GRID_W, h, dh)
    vg = v.reshape(b, rows, GRID_W, h, dh)
    qg = jnp.moveaxis(q.reshape(b, rows, GRID_W, h, dh), 1, 0)
    row_start = jnp.clip(jnp.arange(rows) - wr // 2, 0, rows - wr)
    col_start = jnp.clip(jnp.arange(GRID_W) - NA_COLS // 2, 0, GRID_W - NA_COLS)
    col_idx = col_start[:, None] + jnp.arange(NA_COLS)[None, :]
    off_c = col_idx - jnp.arange(GRID_W)[:, None] + (NA_COLS - 1)
    scale = dh ** -0.5

    def row_block(args):
        r, q_row = args
        rs = row_start[r]
        k_win = lax.dynamic_slice_in_dim(kg, rs, wr, axis=1)[:, :, col_idx]
        v_win = lax.dynamic_slice_in_dim(vg, rs, wr, axis=1)[:, :, col_idx]
        off_r = rs + jnp.arange(wr) - r + (NA_ROWS_MAX - 1)
        bias = rel_bias[:, off_r[:, None, None], off_c[None, :, :]]
        s_win = jnp.einsum('bjhd,bajchd->bhjac', q_row, k_win,
                           preferred_element_type=jnp.float32) * scale
        s_win = s_win + jnp.transpose(bias, (0, 2, 1, 3))[None].astype(jnp.float32)
        s_ctx = jnp.einsum('bjhd,bnhd->bhjn', q_row, k_ctx,
                           preferred_element_type=jnp.float32) * scale
        scores = jnp.concatenate([s_win.reshape(b, h, GRID_W, n_win), s_ctx], axis=-1)
        p = jax.nn.softmax(scores, axis=-1).astype(v.dtype)
        p_win = p[..., :n_win].reshape(b, h, GRID_W, wr, NA_COLS)
        p_ctx = p[..., n_win:]
        return (jnp.einsum('bhjac,bajchd->bjhd', p_win, v_win)
                + jnp.einsum('bhjn,bnhd->bjhd', p_ctx, v_ctx))

    out = lax.map(row_block, (jnp.arange(rows), qg))
    return jnp.moveaxis(out, 0, 1).reshape(b, s, h, dh)


def split_in_proj(p):
    offsets = []
    acc = 0
    for size in IN_SIZES[:-1]:
        acc += size
        offsets.append(acc)
    return jnp.split(p, offsets, axis=-1)


def heads(t, h, d):
    return t.reshape(t.shape[0], t.shape[1], h, d)


def mixer_a(xq, xk, xv, gq, gk, gv, q_gain, k_gain, out_gain, row, col, with_ctx):
    b, s, _ = xq.shape
    n = gq.shape[1]
    lq = rope_2d(rms_norm(heads(xq, A_HEADS, HEAD_DIM), q_gain), row, col)
    lk = rope_2d(rms_norm(heads(xk, A_KV_HEADS, HEAD_DIM), k_gain), row, col)
    ck = rms_norm(heads(gk, A_KV_HEADS, HEAD_DIM), k_gain)
    cv = heads(gv, A_KV_HEADS, HEAD_DIM)
    k_all = jnp.concatenate([ck, lk], axis=1)
    v_all = jnp.concatenate([cv, heads(xv, A_KV_HEADS, HEAD_DIM)], axis=1)
    lat = sweep_query_blocks(lambda qb: gqa_attend(qb, k_all, v_all), lq)
    lat = rms_norm(lat.reshape(b, s, A_WIDTH), out_gain)
    ctx_out = None
    if with_ctx:
        cq = rms_norm(heads(gq, A_HEADS, HEAD_DIM), q_gain)
        ctx_out = rms_norm(gqa_attend(cq, ck, cv).reshape(b, n, A_WIDTH), out_gain)
    return lat, ctx_out


def mixer_b(xq, xk, xv, gq, gk, gv, q_gain, k_gain, lam_vecs, out_gain, lam_init, row, col, with_ctx):
    b, s, _ = xq.shape
    n = gq.shape[1]

    def qk_heads(t):
        return t.reshape(t.shape[0], t.shape[1], B_HEADS, 2, B_SUB_DIM)

    def rope_sub(t):
        return rope_2d(t.reshape(b, s, 2 * B_HEADS, B_SUB_DIM), row, col).reshape(b, s, B_HEADS, 2, B_SUB_DIM)

    lq = rope_sub(rms_norm(qk_heads(xq), q_gain))
    lk = rope_sub(rms_norm(qk_heads(xk), k_gain))
    ck = rms_norm(qk_heads(gk), k_gain)
    cv = heads(gv, B_HEADS, HEAD_DIM)
    k_all = jnp.concatenate([ck, lk], axis=1)
    v_all = jnp.concatenate([cv, heads(xv, B_HEADS, HEAD_DIM)], axis=1)
    lv = lam_vecs.astype(jnp.float32)
    lam = jnp.exp(jnp.sum(lv[0] * lv[1])) - jnp.exp(jnp.sum(lv[2] * lv[3])) + lam_init
    lat = sweep_query_blocks(lambda qb: diff_attend(qb, k_all, v_all, lam), lq)
    lat = (rms_norm(lat, out_gain) * (1.0 - lam_init)).reshape(b, s, B_WIDTH)
    ctx_out = None
    if with_ctx:
        cq = rms_norm(qk_heads(gq), q_gain)
        co = diff_attend(cq, ck, cv, lam)
        ctx_out = (rms_norm(co, out_gain) * (1.0 - lam_init)).reshape(b, n, B_WIDTH)
    return lat, ctx_out


def mixer_c(xq, xk, xv, gq, gk, gv, q_gain, k_gain, rel_bias, out_gain, with_ctx):
    b, s, _ = xq.shape
    n = gq.shape[1]
    lq = rms_norm(heads(xq, C_HEADS, HEAD_DIM), q_gain)
    lk = rms_norm(heads(xk, C_HEADS, HEAD_DIM), k_gain)
    ck = rms_norm(heads(gk, C_HEADS, HEAD_DIM), k_gain)
    cv = heads(gv, C_HEADS, HEAD_DIM)
    lat = neighborhood_attend(lq, lk, heads(xv, C_HEADS, HEAD_DIM), ck, cv, rel_bias)
    lat = rms_norm(lat.reshape(b, s, C_WIDTH), out_gain)
    ctx_out = None
    if with_ctx:
        cq = rms_norm(heads(gq, C_HEADS, HEAD_DIM), q_gain)
        ctx_out = rms_norm(gqa_attend(cq, ck, cv).reshape(b, n, C_WIDTH), out_gain)
    return lat, ctx_out


def setup_inputs(seed: int = 0) -> dict:
    key = jax.random.key(seed)
    ks = jax.random.split(key, 24)
    L, D, F = DEPTH, D_MODEL, FFN_HIDDEN
    nrm = jax.random.normal
    f32 = jnp.float32

    def gain(k, shape):
        return 1.0 + 0.1 * nrm(k, shape, f32)

    return {
        "x": nrm(ks[0], (BATCH, SEQ, D), f32),
        "c": nrm(ks[1], (BATCH, D), f32),
        "ctx": nrm(ks[2], (BATCH, CTX_LEN, D), f32),
        "c_ctx": nrm(ks[3], (D,), f32),
        "w_mod": nrm(ks[4], (L, D, N_MOD * D), f32) * (0.5 * D ** -0.5),
        "b_mod": nrm(ks[5], (L, N_MOD * D), f32) * 0.02,
        "ffn1_w_gu": nrm(ks[6], (L, D, 2 * F), f32) * (D ** -0.5),
        "ffn1_w_down": nrm(ks[7], (L, F, D), f32) * (F ** -0.5),
        "ffn2_w_gu": nrm(ks[8], (L, D, 2 * F), f32) * (D ** -0.5),
        "ffn2_w_down": nrm(ks[9], (L, F, D), f32) * (F ** -0.5),
        "w_in": nrm(ks[10], (L, D, IN_WIDTH), f32) * (D ** -0.5),
        "w_out": nrm(ks[11], (L, MIX_WIDTH, D), f32) * (MIX_WIDTH ** -0.5),
        "a_q_gain": gain(ks[12], (L, HEAD_DIM)),
        "a_k_gain": gain(ks[13], (L, HEAD_DIM)),
        "a_out_gain": gain(ks[14], (L, A_WIDTH)),
        "b_q_gain": gain(ks[15], (L, B_SUB_DIM)),
        "b_k_gain": gain(ks[16], (L, B_SUB_DIM)),
        "b_lambda": nrm(ks[17], (L, 4, B_SUB_DIM), f32) * 0.1,
        "b_out_gain": gain(ks[18], (L, HEAD_DIM)),
        "c_q_gain": gain(ks[19], (L, HEAD_DIM)),
        "c_k_gain": gain(ks[20], (L, HEAD_DIM)),
        "c_rel_bias": nrm(ks[21], (L, C_HEADS, 2 * NA_ROWS_MAX - 1, 2 * NA_COLS - 1), f32) * 0.1,
        "c_out_gain": gain(ks[22], (L, C_WIDTH)),
    }


def reference(x, c, ctx, c_ctx, w_mod, b_mod, ffn1_w_gu, ffn1_w_down, ffn2_w_gu, ffn2_w_down,
              w_in, w_out, a_q_gain, a_k_gain, a_out_gain, b_q_gain, b_k_gain, b_lambda, b_out_gain,
              c_q_gain, c_k_gain, c_rel_bias, c_out_gain):
    b, s, d = x.shape
    t = jnp.arange(s, dtype=jnp.int32)
    row = t // GRID_W
    col = t % GRID_W
    g = ctx
    for l in range(DEPTH):
        with_ctx = l < DEPTH - 1
        mx = (jax.nn.silu(c) @ w_mod[l] + b_mod[l]).reshape(b, N_MOD, 1, d)
        mg = (jax.nn.silu(c_ctx) @ w_mod[l] + b_mod[l]).reshape(N_MOD, 1, 1, d)

        x = x + 0.5 * mx[:, 2] * swiglu(modulate(rms_norm(x), mx[:, 0], mx[:, 1]), ffn1_w_gu[l], ffn1_w_down[l])
        g = g + 0.5 * mg[2] * swiglu(modulate(rms_norm(g), mg[0], mg[1]), ffn1_w_gu[l], ffn1_w_down[l])

        xn = modulate(rms_norm(x), mx[:, 3], mx[:, 4])
        gn = modulate(rms_norm(g), mg[3], mg[4])
        px = split_in_proj(xn @ w_in[l])
        pg = split_in_proj(gn @ w_in[l])
        lam_init = 0.8 - 0.6 * math.exp(-0.3 * l)
        a_lat, a_ctx = mixer_a(px[0], px[1], px[2], pg[0], pg[1], pg[2],
                               a_q_gain[l], a_k_gain[l], a_out_gain[l], row, col, with_ctx)
        b_lat, b_ctx = mixer_b(px[3], px[4], px[5], pg[3], pg[4], pg[5],
                               b_q_gain[l], b_k_gain[l], b_lambda[l], b_out_gain[l], lam_init, row, col, with_ctx)
        c_lat, c_ctx_o = mixer_c(px[6], px[7], px[8], pg[6], pg[7], pg[8],
                                 c_q_gain[l], c_k_gain[l], c_rel_bias[l], c_out_gain[l], with_ctx)
        x = x + mx[:, 5] * (jnp.concatenate([a_lat, b_lat, c_lat], axis=-1) @ w_out[l])
        x = x + 0.5 * mx[:, 8] * swiglu(modulate(rms_norm(x), mx[:, 6], mx[:, 7]), ffn2_w_gu[l], ffn2_w_down[l])
        if with_ctx:
            g = g + mg[5] * (jnp.concatenate([a_ctx, b_ctx, c_ctx_o], axis=-1) @ w_out[l])
            g = g + 0.5 * mg[8] * swiglu(modulate(rms_norm(g), mg[6], mg[7]), ffn2_w_gu[l], ffn2_w_down[l])
    return x
```

```python
import contextlib
import math
import numpy as np
import ml_dtypes
import concourse.bass as bass
import concourse.mybir as mybir
from concourse.bass_utils import run_bass_kernel_spmd

F32 = mybir.dt.float32
BF16 = mybir.dt.bfloat16
AF = mybir.ActivationFunctionType
ALU = mybir.AluOpType
AX = mybir.AxisListType

D = 2048
KC = 16
TL = 1024
TCX = 64
T = TL + TCX
FH = 5632
FCH = 44
NQT = 4
FQ = 11
NMC = 144
EPS = 1e-6
THETA = 10000.0
NEG = -30000.0
NPAR = 420
P_BMOD, P_AQG, P_AKG, P_BQG, P_BKG, P_CQG, P_CKG, P_AOG, P_BOG, P_COG, P_LAM = 0, 144, 145, 146, 147, 148, 149, 150, 156, 157, 163
WT = 2048
NW = 8

NSEM = 4
NDSEM = 12
STRICT = True


class EngQ:
    def __init__(self, name):
        self.name = name
        self.ops = []
        self.n = 0
        self.nd = 0
        self.seen = {}
        self.seen_d = {}
        self.sems = None
        self.dsems = None


class Buf:
    __slots__ = ("w", "r")

    def __init__(self):
        self.w = None
        self.r = {}


def _tkey(t):
    return t[1] if t[0] == "c" else (t[1], t[2])


class Prog:
    def __init__(self, nc, es):
        self.nc = nc
        self.q = {n: EngQ(n) for n in ("pe", "act", "dve", "pool", "sp")}
        for name, q in self.q.items():
            q.sems = [es.enter_context(nc.semaphore(f"s_{name}{i}")) for i in range(NSEM)]
            q.dsems = [es.enter_context(nc.semaphore(f"d_{name}{i}")) for i in range(NDSEM)]
        self.last_d = {}

    def _waits_for(self, q, deps):
        waits = []
        for t in deps:
            if t is None:
                continue
            if t[0] == "c":
                _, src, idx = t
                if q.seen.get(src, 0) >= idx:
                    continue
                q.seen[src] = idx
                waits.append(("c", src, (idx - 1) % NSEM, (idx - 1) // NSEM + 1))
            else:
                _, src, si, val = t
                if q.seen_d.get((src, si), 0) >= val:
                    continue
                q.seen_d[(src, si)] = val
                waits.append(("d", src, si, val))
        return waits

    @staticmethod
    def _deps(reads, writes, eng=None):
        deps = [b.w for b in reads]
        for b in writes:
            for t in [b.w] + list(b.r.values()):
                if (not STRICT) and t is not None and eng is not None and t[0] == "c" and t[1] == eng:
                    continue
                deps.append(t)
        return deps

    @staticmethod
    def _note(tk, reads, writes):
        k = _tkey(tk)
        for b in reads:
            b.r[k] = tk
        for b in writes:
            b.w = tk
            b.r = {}

    def group(self, eng, fns, reads=(), writes=(), deps=()):
        q = self.q[eng]
        waits = self._waits_for(q, self._deps(reads, writes, eng) + list(deps))
        q.n += 1
        tk = ("c", eng, q.n)
        inc = ("c", (q.n - 1) % NSEM)
        for i, fn in enumerate(fns):
            q.ops.append((waits if i == 0 else [], fn, inc if i == len(fns) - 1 else None))
        self._note(tk, reads, writes)
        return tk

    def op(self, eng, fn, reads=(), writes=(), deps=()):
        return self.group(eng, [fn], reads, writes, deps)

    def dma(self, eng, fn, reads=(), writes=(), deps=()):
        q = self.q[eng]
        i = q.nd
        q.nd += 1
        si = i % NDSEM
        val = 16 * (i // NDSEM + 1)
        deps = self._deps(reads, writes) + list(deps)
        if i >= NDSEM:
            deps.append(("d", eng, si, val - 16))
        waits = self._waits_for(q, deps)
        q.ops.append((waits, fn, ("d", si)))
        tk = ("d", eng, si, val)
        self.last_d[(eng, si)] = tk
        self._note(tk, reads, writes)
        return tk

    def all_tickets(self):
        ts = [("c", n, q.n) for n, q in self.q.items() if q.n > 0]
        ts.extend(self.last_d.values())
        return ts

    def barrier(self):
        ts = self.all_tickets()
        for n, q in self.q.items():
            w = self._waits_for(q, ts)
            if w:
                q.ops.append((w, None, None))

    def flush(self, final=False):
        nc = self.nc
        Q = self.q
        if final:
            spq = Q["sp"]
            fw = self._waits_for(spq, self.all_tickets())
            spq.ops.append((fw, None, None))
        with nc.Block() as block:
            def replay(name, e):
                q = Q[name]
                for waits, fn, inc in q.ops:
                    for w in waits:
                        if w[0] == "c":
                            e.wait_ge(Q[w[1]].sems[w[2]], w[3])
                        else:
                            e.wait_ge(Q[w[1]].dsems[w[2]], w[3])
                    if fn is None:
                        continue
                    ins = fn(e)
                    if inc is not None:
                        if inc[0] == "c":
                            ins.then_inc(q.sems[inc[1]], 1)
                        else:
                            ins.then_inc(q.dsems[inc[1]], 16)
                q.ops = []

            @block.tensor
            def _(e):
                replay("pe", e)

            @block.scalar
            def _(e):
                replay("act", e)

            @block.vector
            def _(e):
                replay("dve", e)

            @block.gpsimd
            def _(e):
                replay("pool", e)

            @block.sync
            def _(e):
                replay("sp", e)


class Arena:
    def __init__(self, nc, es, words):
        self.t = es.enter_context(nc.sbuf_tensor("arena", [128, words], F32))
        self.words = words
        self.off = 0

    def f32(self, n):
        assert self.off + n <= self.words, ("SBUF arena overflow", self.off, n, self.words)
        ap = self.t[:, self.off:self.off + n]
        self.off += n
        return ap

    def bf16(self, n):
        w = (n + 1) // 2
        return self.f32(w).bitcast(BF16)

    def mark(self):
        return self.off

    def reset(self, m):
        self.off = m


class Ctx:
    pass


def make_tiles(with_ctx):
    t = [(0, 512, 0), (512, 512, 0)]
    if with_ctx:
        t.append((TL, TCX, 1))
    return t


def setup_common(nc, es, arena_words=50000):
    C = Ctx()
    C.nc = nc
    C.P = Prog(nc, es)
    C.ar = Arena(nc, es, arena_words)
    C.ps = es.enter_context(nc.psum_tensor("ps", [128, 8, 512], F32))
    C.bank = [Buf() for _ in range(8)]
    C.small = [Buf() for _ in range(8)]
    C.unit = 0
    return C


def ps_tile(C, slot, ti, n):
    if ti < 2:
        return C.ps[:, 2 * slot + ti, 0:n], C.bank[2 * slot + ti]
    return C.ps[:, 4 + slot, 0:n], C.bank[4 + slot]


def setup_wring(C):
    C.wslots = [C.ar.bf16(WT) for _ in range(NW)]
    C.wbuf = [Buf() for _ in range(NW)]
    C.wi = 0


def load_w(C, dram_ap, ncols=WT):
    s = C.wi % NW
    C.wi += 1
    dst = C.wslots[s][:, 0:ncols]
    C.P.dma("pool", lambda e, dst=dst, src=dram_ap: e.dma_start(out=dst, in_=src), writes=[C.wbuf[s]])
    return C.wslots[s], C.wbuf[s]


class WStream:
    def __init__(self, C, tiles):
        self.C = C
        self.tiles = tiles
        self.issued = 0
        self.cur = 0
        self.slots = {}

    def get(self):
        while self.issued < len(self.tiles) and self.issued < self.cur + NW - 4:
            ap, ncols = self.tiles[self.issued]
            self.slots[self.issued] = load_w(self.C, ap, ncols)
            self.issued += 1
        r = self.slots.pop(self.cur)
        self.cur += 1
        return r


def mm_unit(C, lhsT_fn, rhs_fn, nk, tiles, reads):
    slot = C.unit % 2
    C.unit += 1
    outs = [ps_tile(C, slot, ti, n) for ti, (c0, n, v) in enumerate(tiles)]
    fns = []
    for k in range(nk):
        for ti, (c0, n, v) in enumerate(tiles):
            fns.append(lambda e, o=outs[ti][0], l=lhsT_fn(k), r=rhs_fn(k, c0, n), st=(k == 0), sp=(k == nk - 1):
                       e.matmul(o, lhsT=l, rhs=r, start=st, stop=sp))
    C.P.group("pe", fns, reads=reads, writes=[o[1] for o in outs])
    return outs


def load_small(C, dram, prm_d, modc_d, cst_d):
    P = C.P
    ar = C.ar
    C.prm = ar.f32(NPAR)
    C.prm_b = Buf()
    P.dma("sp", lambda e: e.dma_start(out=C.prm, in_=prm_d[:, :]), writes=[C.prm_b])
    C.cst = ar.bf16(512)
    C.cst_b = Buf()
    P.dma("pool", lambda e: e.dma_start(out=C.cst, in_=cst_d[:, :]), writes=[C.cst_b])
    C.rotA = C.cst[:, 0:128]
    C.rotB = C.cst[:, 128:256]
    C.ones = C.cst[:, 256:384]
    C.onesB = C.cst[:, 384:512]
    if modc_d is not None:
        C.ms = ar.f32(NMC * 2).rearrange("p (j c v) -> p j c v", j=9, c=16)
        C.ms_b = Buf()
        P.dma("sp", lambda e: e.dma_start(out=C.ms, in_=modc_d.rearrange("p (j c v) -> p j c v", j=9, c=16)), writes=[C.ms_b])
        for j in (1, 4, 7):
            P.op("dve", lambda e, j=j: e.tensor_scalar(out=C.ms[:, j], in0=C.ms[:, j], scalar1=1.0, scalar2=None, op0=ALU.add),
                 reads=[C.ms_b], writes=[C.ms_b])
        for j in (2, 8):
            P.op("dve", lambda e, j=j: e.tensor_scalar(out=C.ms[:, j], in0=C.ms[:, j], scalar1=0.5, scalar2=None, op0=ALU.mult),
                 reads=[C.ms_b], writes=[C.ms_b])


def norm_mod(C, jshift, jscale, tiles):
    P = C.P
    for kc in range(KC):
        for (c0, n, v) in tiles:
            P.op("act", lambda e, kc=kc, c0=c0, n=n: e.activation(out=C.h[:, kc, c0:c0 + n], in_=C.x[:, kc, c0:c0 + n], func=AF.Square),
                 reads=[C.x_b], writes=[C.h_b])
    outs = mm_unit(C, lambda k: C.ones, lambda k, c0, n: C.h[:, k, c0:c0 + n], KC, tiles, [C.h_b, C.cst_b])
    for ti, (c0, n, v) in enumerate(tiles):
        o, ob = outs[ti]
        P.op("act", lambda e, o=o, c0=c0, n=n: e.activation(out=C.rstd[:, c0:c0 + n], in_=o, func=AF.Ln, scale=1.0 / D, bias=EPS),
             reads=[ob], writes=[C.rstd_b])
        P.op("act", lambda e, c0=c0, n=n: e.activation(out=C.rstd[:, c0:c0 + n], in_=C.rstd[:, c0:c0 + n], func=AF.Exp, scale=-0.5),
             reads=[C.rstd_b], writes=[C.rstd_b])
    i = 0
    for kc in range(KC):
        for (c0, n, v) in tiles:
            tb = i % 2
            i += 1
            P.op("dve", lambda e, kc=kc, c0=c0, n=n, tb=tb: e.tensor_tensor(out=C.tmp[tb][:, 0:n], in0=C.x[:, kc, c0:c0 + n], in1=C.rstd[:, c0:c0 + n], op=ALU.mult),
                 reads=[C.x_b, C.rstd_b], writes=[C.tmp_b[tb]])
            P.op("act", lambda e, kc=kc, c0=c0, n=n, tb=tb, v=v: e.activation(out=C.h[:, kc, c0:c0 + n], in_=C.tmp[tb][:, 0:n], func=AF.Identity,
                                                                         scale=C.ms[:, jscale, kc, v:v + 1], bias=C.ms[:, jshift, kc, v:v + 1]),
                 reads=[C.tmp_b[tb], C.ms_b], writes=[C.h_b])


def ffn_wtiles(wgu_d, wdn_d):
    lst = []
    for qt in range(NQT):
        for fi in range(FQ):
            f = qt * FQ + fi
            for half in range(2):
                lst.append((wgu_d[f * 2 + half, :, :], WT))
        for d in range(KC):
            lst.append((wdn_d[qt * KC + d, :, :], FQ * 128))
    return lst


def ffn(C, jgate, tiles):
    P = C.P
    for qt in range(NQT):
        for fi in range(FQ):
            f = qt * FQ + fi
            res = []
            for half in range(2):
                ws, wb = C.wst.get()
                outs = mm_unit(C, lambda k, ws=ws: ws[:, k * 128:(k + 1) * 128], lambda k, c0, n: C.h[:, k, c0:c0 + n], KC, tiles, [wb, C.h_b])
                res.append(outs)
            for ti, (c0, n, v) in enumerate(tiles):
                g, gb = res[0][ti]
                u, ub = res[1][ti]
                sb = C.sgi % 3
                C.sgi += 1
                P.op("act", lambda e, g=g, n=n, sb=sb: e.activation(out=C.sg[sb][:, 0:n], in_=g, func=AF.Silu),
                     reads=[gb], writes=[C.sg_b[sb]])
                P.op("dve", lambda e, u=u, n=n, sb=sb, fi=fi, c0=c0: e.tensor_tensor(out=C.act[:, fi, c0:c0 + n], in0=C.sg[sb][:, 0:n], in1=u, op=ALU.mult),
                     reads=[ub, C.sg_b[sb]], writes=[C.act_b])
        for d in range(KC):
            ws, wb = C.wst.get()
            outs = mm_unit(C, lambda k, ws=ws: ws[:, k * 128:(k + 1) * 128], lambda k, c0, n: C.act[:, k, c0:c0 + n], FQ, tiles, [wb, C.act_b])
            for ti, (c0, n, v) in enumerate(tiles):
                y, yb = outs[ti]
                P.op("dve", lambda e, y=y, d=d, c0=c0, n=n, v=v: e.scalar_tensor_tensor(out=C.x[:, d, c0:c0 + n], in0=y, scalar=C.ms[:, jgate, d, v:v + 1],
                                                                                      in1=C.x[:, d, c0:c0 + n], op0=ALU.mult, op1=ALU.add),
                     reads=[yb, C.ms_b, C.x_b], writes=[C.x_b])


def alloc_ffn_bufs(C):
    ar = C.ar
    C.act = ar.bf16(FQ * T).rearrange("p (k t) -> p k t", t=T)
    C.act_b = Buf()
    C.sg = [ar.f32(512) for _ in range(3)]
    C.sg_b = [Buf() for _ in range(3)]
    C.sgi = 0


def alloc_norm_bufs(C):
    ar = C.ar
    C.rstd = ar.f32(T)
    C.rstd_b = Buf()
    C.tmp = [ar.f32(512) for _ in range(2)]
    C.tmp_b = [Buf() for _ in range(2)]


NM_PER = 2 * NMC // 8


def build_M():
    nc = bass.Bass("TRN2", target_bir_lowering=False)
    cv_d = nc.dram_tensor("cvec", [128, KC * 3], F32, kind="ExternalInput").ap()
    wm_d = nc.dram_tensor("wmod", [NM_PER, 128, WT], F32, kind="ExternalInput").ap()
    bm_d = nc.dram_tensor("bmod", [128, NM_PER], F32, kind="ExternalInput").ap()
    out_d = nc.dram_tensor("mout", [128, NM_PER * 3], F32, kind="ExternalOutput").ap()
    with contextlib.ExitStack() as es:
        C = setup_common(nc, es, 20000)
        P, ar = C.P, C.ar
        setup_wring(C)
        cv = ar.f32(KC * 3)
        cvs = ar.f32(KC * 3)
        cvb = ar.bf16(KC * 3)
        bm = ar.f32(NM_PER)
        res = ar.f32(NM_PER * 3)
        b_cv, b_cvb, b_bm, b_res = Buf(), Buf(), Buf(), Buf()
        P.dma("sp", lambda e: e.dma_start(out=cv, in_=cv_d[:, :]), writes=[b_cv])
        P.dma("sp", lambda e: e.dma_start(out=bm, in_=bm_d[:, :]), writes=[b_bm])
        b_cvs = Buf()
        P.op("act", lambda e: e.activation(out=cvs, in_=cv, func=AF.Silu), reads=[b_cv], writes=[b_cvs])
        P.op("dve", lambda e: e.tensor_copy(out=cvb, in_=cvs), reads=[b_cvs], writes=[b_cvb])
        pm = C.ps[:, 7, 0:NM_PER * 3]
        C.wst = WStream(C, [(wm_d[t, :, :], WT) for t in range(NM_PER)])
        for t in range(NM_PER):
            ws, wb = C.wst.get()
            fns = []
            for k in range(KC):
                fns.append(lambda e, t=t, k=k, ws=ws: e.matmul(pm[:, t * 3:(t + 1) * 3], lhsT=ws[:, k * 128:(k + 1) * 128], rhs=cvb[:, k * 3:(k + 1) * 3],
                                                         start=(k == 0), stop=(k == KC - 1)))
            P.group("pe", fns, reads=[wb, b_cvb], writes=[C.bank[7]])
        pm3 = pm.rearrange("p (t v) -> p t v", v=3)
        res3 = res.rearrange("p (t v) -> p t v", v=3)
        for v in range(3):
            P.op("dve", lambda e, v=v: e.tensor_tensor(out=res3[:, :, v], in0=pm3[:, :, v], in1=bm, op=ALU.add),
                 reads=[C.bank[7], b_bm], writes=[b_res])
        P.dma("sp", lambda e: e.dma_start(out=out_d[:, :], in_=res), reads=[b_res])
        P.flush(final=True)
    return nc


def qk_chunks():
    lst = []
    for i in range(6):
        lst.append(("q", i, P_AQG, "A"))
    for i in range(2):
        lst.append(("k", i, P_AKG, "A"))
    for i in range(4):
        lst.append(("q", 6 + i, P_BQG, "B"))
    for i in range(4):
        lst.append(("k", 2 + i, P_BKG, "B"))
    for i in range(6):
        lst.append(("q", 10 + i, P_CQG, None))
    for i in range(6):
        lst.append(("k", 6 + i, P_CKG, None))
    return lst


def build_A():
    nc = bass.Bass("TRN2", target_bir_lowering=False)
    x_d = nc.dram_tensor("xT", [128, KC * T], F32, kind="ExternalInput").ap()
    modc_d = nc.dram_tensor("modc", [128, NMC * 2], F32, kind="ExternalInput").ap()
    prm_d = nc.dram_tensor("prm", [128, NPAR], F32, kind="ExternalInput").ap()
    cst_d = nc.dram_tensor("cst", [128, 512], F32, kind="ExternalInput").ap()
    rope_d = nc.dram_tensor("rope", [128, 4 * TL], F32, kind="ExternalInput").ap()
    wgu_d = nc.dram_tensor("wgu", [2 * FCH, 128, WT], F32, kind="ExternalInput").ap()
    wdn_d = nc.dram_tensor("wdn", [NQT * KC, 128, FQ * 128], F32, kind="ExternalInput").ap()
    wqk_d = nc.dram_tensor("wqk", [28, 128, WT], F32, kind="ExternalInput").ap()
    wv_d = nc.dram_tensor("wv", [12, 128, WT], F32, kind="ExternalInput").ap()
    xo_d = nc.dram_tensor("xTo", [128, KC * T], F32, kind="ExternalOutput").ap()
    q_d = nc.dram_tensor("qo", [16, 128, T], BF16, kind="ExternalOutput").ap()
    k_d = nc.dram_tensor("ko", [12, 128, T], BF16, kind="ExternalOutput").ap()
    v_d = nc.dram_tensor("vo", [T, 1536], BF16, kind="ExternalOutput").ap()
    tiles = make_tiles(True)
    with contextlib.ExitStack() as es:
        C = setup_common(nc, es)
        P, ar = C.P, C.ar
        C.x = ar.f32(KC * T).rearrange("p (k t) -> p k t", t=T)
        C.x_b = Buf()
        P.dma("sp", lambda e: e.dma_start(out=C.x, in_=x_d.rearrange("p (k t) -> p k t", t=T)), writes=[C.x_b])
        load_small(C, None, prm_d, modc_d, cst_d)
        C.h = ar.bf16(KC * T).rearrange("p (k t) -> p k t", t=T)
        C.h_b = Buf()
        setup_wring(C)
        alloc_norm_bufs(C)
        m0 = ar.mark()
        alloc_ffn_bufs(C)
        C.wst = WStream(C, ffn_wtiles(wgu_d, wdn_d) + [(wqk_d[i, :, :], WT) for i in range(28)] + [(wv_d[i, :, :], WT) for i in range(12)])
        norm_mod(C, 0, 1, tiles)
        ffn(C, 2, tiles)
        P.dma("sp", lambda e: e.dma_start(out=xo_d.rearrange("p (k t) -> p k t", t=T), in_=C.x), reads=[C.x_b])
        norm_mod(C, 3, 4, tiles)
        P.barrier()
        P.flush()
        ar.reset(m0)
        rope = ar.f32(4 * TL)
        rope_b = Buf()
        P.dma("sp", lambda e: e.dma_start(out=rope, in_=rope_d[:, :]), writes=[rope_b])
        tabs = {"A": (rope[:, 0:TL], rope[:, TL:2 * TL]), "B": (rope[:, 2 * TL:3 * TL], rope[:, 3 * TL:4 * TL])}
        stage = [ar.bf16(T) for _ in range(2)]
        stage_b = [Buf() for _ in range(2)]
        sqb = [ar.bf16(512) for _ in range(2)]
        sqb_b = [Buf() for _ in range(2)]
        qgb = [ar.bf16(512) for _ in range(2)]
        qgb_b = [Buf() for _ in range(2)]
        rsb = [ar.f32(512) for _ in range(2)]
        rsb_b = [Buf() for _ in range(2)]
        t1 = [ar.f32(512) for _ in range(2)]
        t1_b = [Buf() for _ in range(2)]
        t2 = [ar.f32(512) for _ in range(2)]
        t2_b = [Buf() for _ in range(2)]
        vst = [ar.bf16(512) for _ in range(2)]
        vst_b = [Buf() for _ in range(2)]
        chunks = qk_chunks()
        cnt = [0]

        def post(ci, outs):
            kind, dci, gcol, rt = chunks[ci]
            sidx = ci % 2
            for ti, (c0, n, v) in enumerate(tiles):
                o, ob = outs[ti]
                i = cnt[0] % 2
                cnt[0] += 1
                dn = 64.0 if rt == "B" else 128.0
                onesm = C.onesB if rt == "B" else C.ones
                P.op("act", lambda e, o=o, n=n, i=i: e.activation(out=sqb[i][:, 0:n], in_=o, func=AF.Square), reads=[ob], writes=[sqb_b[i]])
                P.op("dve", lambda e, o=o, n=n, i=i: e.tensor_scalar(out=qgb[i][:, 0:n], in0=o, scalar1=C.prm[:, gcol:gcol + 1], scalar2=None, op0=ALU.mult),
                     reads=[ob, C.prm_b], writes=[qgb_b[i]])
                ssum = C.ps[:, 7, 0:n]
                P.op("pe", lambda e, n=n, i=i, ssum=ssum, onesm=onesm: e.matmul(ssum, lhsT=onesm, rhs=sqb[i][:, 0:n], start=True, stop=True),
                     reads=[sqb_b[i], C.cst_b], writes=[C.bank[7]])
                P.op("act", lambda e, n=n, i=i, ssum=ssum, dn=dn: e.activation(out=rsb[i][:, 0:n], in_=ssum, func=AF.Ln, scale=1.0 / dn, bias=EPS),
                     reads=[C.bank[7]], writes=[rsb_b[i]])
                P.op("act", lambda e, n=n, i=i: e.activation(out=rsb[i][:, 0:n], in_=rsb[i][:, 0:n], func=AF.Exp, scale=-0.5), reads=[rsb_b[i]], writes=[rsb_b[i]])
                dst = stage[sidx][:, c0:c0 + n]
                if rt is not None and v == 0:
                    rotm = C.rotA if rt == "A" else C.rotB
                    cs, sn = tabs[rt]
                    rot = C.psr[:, 0:n]
                    P.op("pe", lambda e, n=n, i=i, rot=rot, rotm=rotm: e.matmul(rot, lhsT=rotm, rhs=qgb[i][:, 0:n], start=True, stop=True),
                         reads=[qgb_b[i], C.cst_b], writes=[C.psr_b])
                    P.op("pool", lambda e, n=n, i=i, c0=c0, cs=cs: e.tensor_tensor(out=t1[i][:, 0:n], in0=qgb[i][:, 0:n], in1=cs[:, c0:c0 + n], op=ALU.mult),
                         reads=[qgb_b[i], rope_b], writes=[t1_b[i]])
                    P.op("dve", lambda e, n=n, i=i, c0=c0, sn=sn, rot=rot: e.tensor_tensor(out=t2[i][:, 0:n], in0=rot, in1=sn[:, c0:c0 + n], op=ALU.mult),
                         reads=[C.psr_b, rope_b], writes=[t2_b[i]])
                    P.op("pool", lambda e, n=n, i=i: e.tensor_tensor(out=t1[i][:, 0:n], in0=t1[i][:, 0:n], in1=t2[i][:, 0:n], op=ALU.add),
                         reads=[t2_b[i], t1_b[i]], writes=[t1_b[i]])
                    P.op("dve", lambda e, n=n, i=i, dst=dst: e.tensor_tensor(out=dst, in0=t1[i][:, 0:n], in1=rsb[i][:, 0:n], op=ALU.mult),
                         reads=[t1_b[i], rsb_b[i]], writes=[stage_b[sidx]])
                else:
                    P.op("dve", lambda e, n=n, i=i, dst=dst: e.tensor_tensor(out=dst, in0=qgb[i][:, 0:n], in1=rsb[i][:, 0:n], op=ALU.mult),
                         reads=[qgb_b[i], rsb_b[i]], writes=[stage_b[sidx]])
            dd = q_d if kind == "q" else k_d
            P.dma("sp", lambda e, dd=dd, dci=dci, sidx=sidx: e.dma_start(out=dd[dci, :, :], in_=stage[sidx]), reads=[stage_b[sidx]])

        C.psr = C.ps[:, 6, :]
        C.psr_b = C.bank[6]
        C.unit = 0

        def unit2(lhsT_fn, rhs_fn, nk, reads):
            slot = C.unit % 2
            C.unit += 1
            outs = [ps_tile(C, slot, ti, n) for ti, (c0, n, v) in enumerate(tiles)]
            fns = []
            for k in range(nk):
                for ti, (c0, n, v) in enumerate(tiles):
                    fns.append(lambda e, o=outs[ti][0], l=lhsT_fn(k), r=rhs_fn(k, c0, n), st=(k == 0), sp=(k == nk - 1):
                               e.matmul(o, lhsT=l, rhs=r, start=st, stop=sp))
            P.group("pe", fns, reads=reads, writes=[o[1] for o in outs])
            return outs

        prev = None
        for ci in range(len(chunks)):
            ws, wb = C.wst.get()
            outs = mm_unit(C, lambda k, ws=ws: ws[:, k * 128:(k + 1) * 128], lambda k, c0, n: C.h[:, k, c0:c0 + n], KC, tiles, [wb, C.h_b])
            if prev is not None:
                post(*prev)
            prev = (ci, outs)
        post(*prev)
        ttiles = [(i * 128, 128) for i in range(8)] + [(TL, TCX)]
        vi = 0
        for g in range(3):
            wl = [C.wst.get() for kq in range(4)]
            for (t0, nt) in ttiles:
                bk = (vi % 2) * 2
                o = C.ps[0:nt, bk, :]
                fns = []
                for k in range(KC):
                    ws = wl[k // 4][0]
                    fns.append(lambda e, o=o, k=k, ws=ws, t0=t0, nt=nt: e.matmul(o, lhsT=C.h[:, k, t0:t0 + nt], rhs=ws[:, (k % 4) * 512:(k % 4 + 1) * 512],
                                                                              start=(k == 0), stop=(k == KC - 1)))
                P.group("pe", fns, reads=[w[1] for w in wl] + [C.h_b], writes=[C.bank[bk]])
                si = vi % 2
                if vi % 2 == 0:
                    P.op("act", lambda e, o=o, nt=nt, si=si: e.activation(out=vst[si][0:nt, :], in_=o, func=AF.Copy), reads=[C.bank[bk]], writes=[vst_b[si]])
                else:
                    P.op("dve", lambda e, o=o, nt=nt, si=si: e.tensor_copy(out=vst[si][0:nt, :], in_=o), reads=[C.bank[bk]], writes=[vst_b[si]])
                P.dma("sp", lambda e, t0=t0, nt=nt, g=g, si=si: e.dma_start(out=v_d[t0:t0 + nt, g * 512:(g + 1) * 512], in_=vst[si][0:nt, :]), reads=[vst_b[si]])
                vi += 1
        P.flush(final=True)
    return nc


def _c(a):
    return np.ascontiguousarray(a, dtype=np.float32)


def prep_layer(inp, l):
    W = {}
    wm = inp["w_mod"][l]
    W["wmod"] = _c(wm.reshape(KC, 128, NMC, 128).transpose(2, 1, 0, 3)).reshape(NMC, 128, WT)
    for nm in ("ffn1", "ffn2"):
        wgu = inp[nm + "_w_gu"][l]
        W[nm + "_gu"] = _c(wgu.reshape(KC, 128, 2, FCH, 128).transpose(3, 2, 1, 0, 4)).reshape(2 * FCH, 128, WT)
        wd = inp[nm + "_w_down"][l]
        W[nm + "_dn"] = _c(wd.reshape(NQT, FQ, 128, KC, 128).transpose(0, 3, 2, 1, 4)).reshape(NQT * KC, 128, FQ * 128)
    win = inp["w_in"][l]
    qk_cols = ([0 + i * 128 for i in range(6)] + [768 + i * 128 for i in range(2)] + [1280 + i * 128 for i in range(4)]
               + [1792 + i * 128 for i in range(4)] + [2816 + i * 128 for i in range(6)] + [3584 + i * 128 for i in range(6)])
    W["wqk"] = _c(np.stack([win[:, c0:c0 + 128].reshape(KC, 128, 128).transpose(1, 0, 2).reshape(128, WT) for c0 in qk_cols]))
    wv = np.concatenate([win[:, 1024:1280], win[:, 2304:2816], win[:, 4352:5120]], axis=1)
    W["wv"] = _c(wv.reshape(4, 4, 128, 3, 512).transpose(3, 0, 2, 1, 4)).reshape(12, 128, WT)
    wo = inp["w_out"][l]
    W["wout"] = _c(wo.reshape(KC, 128, KC, 128).transpose(2, 1, 0, 3)).reshape(KC, 128, WT)
    prm = np.zeros((128, NPAR), np.float32)
    prm[:, P_BMOD:P_BMOD + NMC] = inp["b_mod"][l].reshape(NMC, 128).T
    prm[:, P_AQG] = inp["a_q_gain"][l]
    prm[:, P_AKG] = inp["a_k_gain"][l]
    prm[:, P_BQG] = np.tile(inp["b_q_gain"][l], 2)
    prm[:, P_BKG] = np.tile(inp["b_k_gain"][l], 2)
    prm[:, P_CQG] = inp["c_q_gain"][l]
    prm[:, P_CKG] = inp["c_k_gain"][l]
    prm[:, P_AOG:P_AOG + 6] = inp["a_out_gain"][l].reshape(6, 128).T
    prm[:, P_BOG] = inp["b_out_gain"][l]
    prm[:, P_COG:P_COG + 6] = inp["c_out_gain"][l].reshape(6, 128).T
    prm[:, P_LAM:P_LAM + 256] = np.broadcast_to(inp["b_lambda"][l].reshape(1, 256), (128, 256))
    W["prm"] = prm
    rb = inp["c_rel_bias"][l]
    j = np.arange(64)
    cs = np.clip(j - 8, 0, 48)
    jp = np.arange(64)[:, None]
    inside = (jp >= cs[None, :]) & (jp < cs[None, :] + 16)
    idx = np.clip(jp - j[None, :] + 15, 0, 30)
    g = rb[:, :, idx]
    tc = np.where(inside[None, None], g, np.float32(NEG)).transpose(2, 0, 1, 3)
    W["tc"] = _c(tc).reshape(64, 6 * 15 * 64)
    return W


def const_tables():
    rotA = np.zeros((128, 128), np.float32)
    rotB = np.zeros((128, 128), np.float32)
    for m in range(128):
        if (m % 64) < 32:
            rotA[m + 32, m] = -1.0
        else:
            rotA[m - 32, m] = 1.0
        if (m % 32) < 16:
            rotB[m + 16, m] = -1.0
        else:
            rotB[m - 16, m] = 1.0
    ones = np.ones((128, 128), np.float32)
    onesB = np.zeros((128, 128), np.float32)
    onesB[0:64, 0:64] = 1.0
    onesB[64:128, 64:128] = 1.0
    return np.concatenate([rotA, rotB, ones, onesB], axis=1)


def rope_tables(qd):
    t = np.arange(TL)
    row = (16 * qd + t // 64).astype(np.float32)
    col = (t % 64).astype(np.float32)
    out = np.zeros((128, 4 * TL), np.float32)
    fA = (np.float32(THETA) ** (-(np.arange(32, dtype=np.float32) / np.float32(32)))).astype(np.float32)
    fB = (np.float32(THETA) ** (-(np.arange(16, dtype=np.float32) / np.float32(16)))).astype(np.float32)
    for p in range(128):
        pos = row if p < 64 else col
        ang = (pos * fA[p % 32]).astype(np.float32)
        out[p, 0:TL] = np.cos(ang)
        out[p, TL:2 * TL] = np.sin(ang)
        w = p % 64
        pos = row if w < 32 else col
        ang = (pos * fB[w % 16]).astype(np.float32)
        out[p, 2 * TL:3 * TL] = np.cos(ang)
        out[p, 3 * TL:4 * TL] = np.sin(ang)
    return out


def to_fm(a):
    n = a.shape[0]
    return a.reshape(n, KC, 128).transpose(2, 1, 0)


_NC_CACHE = {}


def get_nc(name, builder):
    if name not in _NC_CACHE:
        _NC_CACHE[name] = builder()
    return _NC_CACHE[name]


def run_M(inp, Ws):
    cv = np.stack([inp["c"][0], inp["c"][1], inp["c_ctx"]], axis=1)
    cvec = _c(cv.reshape(KC, 128, 3).transpose(1, 0, 2)).reshape(128, KC * 3)
    wall = np.concatenate([Ws[0]["wmod"], Ws[1]["wmod"]], axis=0)
    ball = np.concatenate([Ws[0]["prm"][:, P_BMOD:P_BMOD + NMC], Ws[1]["prm"][:, P_BMOD:P_BMOD + NMC]], axis=1)
    in_maps = []
    for i in range(8):
        in_maps.append({"cvec": cvec, "wmod": _c(wall[i * NM_PER:(i + 1) * NM_PER]), "bmod": _c(ball[:, i * NM_PER:(i + 1) * NM_PER])})
    res = run_bass_kernel_spmd(get_nc("M", build_M), in_maps, core_ids=list(range(8)))
    mod = np.concatenate([r["mout"].reshape(128, NM_PER, 3) for r in res.results], axis=1)
    return mod.reshape(128, 2, NMC, 3)


def modc_for(mod, l, b):
    return _c(mod[:, l][:, :, [b, 2]]).reshape(128, NMC * 2)


NSLOT = 24
CK = NSLOT * 64 + 256


def attend1(C, qap, qb, nq, chunks, scale, dst, dst_b, kvb, ob, sbk):
    P = C.P
    O = C.ps[:, ob, 0:nq]
    S = C.ps[:, sbk, 0:nq]
    n = len(chunks)

    def pv(i, E, ei, vap, nk):
        P.group("pe", [lambda e: e.matmul(O, lhsT=vap, rhs=E, start=(i == 0), stop=(i == n - 1)),
                       lambda e: e.matmul(S, lhsT=C.ones[0:nk, :], rhs=E, start=(i == 0), stop=(i == n - 1))],
                reads=[C.E_b[ei], kvb, C.cst_b], writes=[C.bank[ob], C.bank[sbk]])
    pend = None
    for i, (kap, vap, nk) in enumerate(chunks):
        si = C.sci % 3
        C.sci += 1
        sc = C.ps[0:nk, si, 0:nq]
        P.op("pe", lambda e, sc=sc, kap=kap: e.matmul(sc, lhsT=kap, rhs=qap, start=True, stop=True), reads=[kvb, qb], writes=[C.bank[si]])
        ei = C.ei % 4
        C.ei += 1
        E = C.E[ei][0:nk, 0:nq]
        P.op("act", lambda e, sc=sc, E=E: e.activation(out=E, in_=sc, func=AF.Exp, scale=scale), reads=[C.bank[si]], writes=[C.E_b[ei]])
        if pend is not None:
            pv(*pend)
        pend = (i, E, ei, vap, nk)
    pv(*pend)
    P.op("dve", lambda e: e.reciprocal(out=C.rc[0][:, 0:nq], in_=S), reads=[C.bank[sbk]], writes=[C.rc_b[0]])
    P.op("dve", lambda e: e.tensor_tensor(out=dst, in0=O, in1=C.rc[0][:, 0:nq], op=ALU.mult), reads=[C.bank[ob], C.rc_b[0]], writes=[dst_b])


def attend2(C, qap, qb, nq, chunks, scale, dst, dst_b, kvb):
    P = C.P
    O1, O2, S1, S2 = (C.ps[:, b, 0:nq] for b in (4, 5, 6, 7))
    n = len(chunks)

    def pv(i, E1, E2, pr, vap, nk):
        st, sp = (i == 0), (i == n - 1)
        P.group("pe", [lambda e: e.matmul(O1, lhsT=vap, rhs=E1, start=st, stop=sp),
                       lambda e: e.matmul(S1, lhsT=C.ones[0:nk, :], rhs=E1, start=st, stop=sp),
                       lambda e: e.matmul(O2, lhsT=vap, rhs=E2, start=st, stop=sp),
                       lambda e: e.matmul(S2, lhsT=C.ones[0:nk, :], rhs=E2, start=st, stop=sp)],
                reads=[C.E_b[2 * pr], C.E_b[2 * pr + 1], kvb, C.cst_b], writes=[C.bank[4], C.bank[5], C.bank[6], C.bank[7]])
    pend = None
    for i, (kap, vap, nk) in enumerate(chunks):
        pr = C.sc2 % 2
        C.sc2 += 1
        sc1 = C.ps[0:nk, 2 * pr, 0:nq]
        sc2 = C.ps[0:nk, 2 * pr + 1, 0:nq]
        P.group("pe", [lambda e, sc1=sc1, kap=kap: e.matmul(sc1, lhsT=kap[0:64, :], rhs=qap[0:64, :], start=True, stop=True),
                       lambda e, sc2=sc2, kap=kap: e.matmul(sc2, lhsT=kap[64:128, :], rhs=qap[64:128, :], start=True, stop=True)],
                reads=[kvb, qb], writes=[C.bank[2 * pr], C.bank[2 * pr + 1]])
        E1 = C.E[2 * pr][0:nk, 0:nq]
        E2 = C.E[2 * pr + 1][0:nk, 0:nq]
        P.op("act", lambda e, sc1=sc1, E1=E1: e.activation(out=E1, in_=sc1, func=AF.Exp, scale=scale), reads=[C.bank[2 * pr]], writes=[C.E_b[2 * pr]])
        P.op("act", lambda e, sc2=sc2, E2=E2: e.activation(out=E2, in_=sc2, func=AF.Exp, scale=scale), reads=[C.bank[2 * pr + 1]], writes=[C.E_b[2 * pr + 1]])
        if pend is not None:
            pv(*pend)
        pend = (i, E1, E2, pr, vap, nk)
    pv(*pend)
    r0, r1, r2 = C.rc[0][:, 0:nq], C.rc[1][:, 0:nq], C.rc[2][:, 0:nq]
    P.op("dve", lambda e: e.reciprocal(out=r0, in_=S1), reads=[C.bank[6]], writes=[C.rc_b[0]])
    P.op("dve", lambda e: e.tensor_tensor(out=r0, in0=O1, in1=r0, op=ALU.mult), reads=[C.bank[4], C.rc_b[0]], writes=[C.rc_b[0]])
    P.op("dve", lambda e: e.reciprocal(out=r1, in_=S2), reads=[C.bank[7]], writes=[C.rc_b[1]])
    P.op("dve", lambda e: e.tensor_tensor(out=r2, in0=O2, in1=r1, op=ALU.mult), reads=[C.bank[5], C.rc_b[1]], writes=[C.rc_b[2]])
    P.op("dve", lambda e: e.scalar_tensor_tensor(out=dst, in0=r2, scalar=C.nlam[:, 0:1], in1=r0, op0=ALU.mult, op1=ALU.add),
         reads=[C.rc_b[2], C.rc_b[0], C.nlam_b], writes=[dst_b])


def build_B(with_ctx, lam_init):
    nc = bass.Bass("TRN2", target_bir_lowering=False)
    x_d = nc.dram_tensor("xT", [128, KC * T], F32, kind="ExternalInput").ap()
    modc_d = nc.dram_tensor("modc", [128, NMC * 2], F32, kind="ExternalInput").ap()
    prm_d = nc.dram_tensor("prm", [128, NPAR], F32, kind="ExternalInput").ap()
    cst_d = nc.dram_tensor("cst", [128, 512], F32, kind="ExternalInput").ap()
    q_d = nc.dram_tensor("q", [16, 128, T], BF16, kind="ExternalInput").ap()
    kg_d = nc.dram_tensor("kg", [48, 128, T], BF16, kind="ExternalInput").ap()
    vg_d = nc.dram_tensor("vg", [4 * T, 1536], BF16, kind="ExternalInput").ap()
    kcw_d = nc.dram_tensor("kcw", [6, 128, CK], BF16, kind="ExternalInput").ap()
    vcw_d = nc.dram_tensor("vcw", [CK, 768], BF16, kind="ExternalInput").ap()
    tcom_d = nc.dram_tensor("tcom", [64, 6 * 15 * 64], F32, kind="ExternalInput").ap()
    tsp_d = nc.dram_tensor("tsp", [64, 6 * 8 * 512], F32, kind="ExternalInput").ap()
    wout_d = nc.dram_tensor("wout", [KC, 128, WT], F32, kind="ExternalInput").ap()
    wgu_d = nc.dram_tensor("wgu", [2 * FCH, 128, WT], F32, kind="ExternalInput").ap()
    wdn_d = nc.dram_tensor("wdn", [NQT * KC, 128, FQ * 128], F32, kind="ExternalInput").ap()
    xo_d = nc.dram_tensor("xTo", [128, KC * T], F32, kind="ExternalOutput").ap()
    tiles = make_tiles(with_ctx)
    qtiles = [(0, 512), (512, 512)]
    with contextlib.ExitStack() as es:
        C = setup_common(nc, es)
        P, ar = C.P, C.ar
        C.x = ar.f32(KC * T).rearrange("p (k t) -> p k t", t=T)
        C.x_b = Buf()
        P.dma("sp", lambda e: e.dma_start(out=C.x, in_=x_d.rearrange("p (k t) -> p k t", t=T)), writes=[C.x_b])
        load_small(C, None, prm_d, modc_d, cst_d)
        lt = ar.f32(128)
        ls = ar.f32(2)
        C.nlam = ar.f32(1)
        C.nlam_b = Buf()
        lt_b, ls_b = Buf(), Buf()
        lam = C.prm[:, P_LAM:P_LAM + 256]
        for i in range(2):
            P.op("dve", lambda e, i=i: e.tensor_tensor(out=lt[:, i * 64:(i + 1) * 64], in0=lam[:, (2 * i) * 64:(2 * i + 1) * 64],
                                                  in1=lam[:, (2 * i + 1) * 64:(2 * i + 2) * 64], op=ALU.mult), reads=[C.prm_b], writes=[lt_b])
            P.op("dve", lambda e, i=i: e.reduce_sum(out=ls[:, i:i + 1], in_=lt[:, i * 64:(i + 1) * 64], axis=AX.X), reads=[lt_b], writes=[ls_b])
        P.op("act", lambda e: e.activation(out=ls, in_=ls, func=AF.Exp), reads=[ls_b], writes=[ls_b])
        P.op("dve", lambda e: e.tensor_tensor(out=C.nlam, in0=ls[:, 1:2], in1=ls[:, 0:1], op=ALU.subtract), reads=[ls_b], writes=[C.nlam_b])
        P.op("dve", lambda e: e.tensor_scalar(out=C.nlam, in0=C.nlam, scalar1=-float(lam_init), scalar2=None, op0=ALU.add), reads=[C.nlam_b], writes=[C.nlam_b])
        mark0 = ar.mark()
        C.ao = ar.bf16(KC * T).rearrange("p (k t) -> p k t", t=T)
        C.ao_b = Buf()
        mark1 = ar.mark()
        C.E = [ar.bf16(512) for _ in range(4)]
        C.E_b = [Buf() for _ in range(4)]
        C.rc = [ar.f32(512) for _ in range(3)]
        C.rc_b = [Buf() for _ in range(3)]
        C.sci = C.ei = C.sc2 = 0
        mark2 = ar.mark()
        qh = [ar.bf16(T) for _ in range(2)]
        qh_b = [Buf() for _ in range(2)]
        kt = [ar.bf16(4 * T) for _ in range(2)]
        vt = [ar.bf16(36 * 128).rearrange("p (c d) -> p c d", d=128) for _ in range(2)]
        kv_b = [Buf() for _ in range(2)]

        def load_kv(slot, kchunk, vcol):
            for r in range(4):
                P.dma("sp", lambda e, r=r: e.dma_start(out=kt[slot][:, r * T:(r + 1) * T], in_=kg_d[r * 12 + kchunk, :, :]), writes=[kv_b[slot]])
                P.dma("sp", lambda e, r=r: e.dma_start(out=vt[slot][:, r * 9:r * 9 + 8, :],
                                                      in_=vg_d[r * T:r * T + TL, vcol:vcol + 128].rearrange("(j p) d -> p j d", p=128)), writes=[kv_b[slot]])
                P.dma("sp", lambda e, r=r: e.dma_start(out=vt[slot][0:64, r * 9 + 8, :], in_=vg_d[r * T + TL:(r + 1) * T, vcol:vcol + 128]), writes=[kv_b[slot]])

        def chunks_for(slot, ctx_only):
            lst = []
            for r in range(4):
                if not ctx_only:
                    for j in range(8):
                        lst.append((kt[slot][:, r * T + j * 128:r * T + (j + 1) * 128], vt[slot][:, r * 9 + j, :], 128))
                lst.append((kt[slot][:, r * T + TL:(r + 1) * T], vt[slot][0:64, r * 9 + 8, :], 64))
            return lst

        qi = 0
        kvi = 0
        obi = 0
        sA = 128.0 ** -0.5
        sB = 64.0 ** -0.5
        for kvh in range(2):
            ks = kvi % 2
            kvi += 1
            load_kv(ks, kvh, kvh * 128)
            for g in range(3):
                hq = kvh * 3 + g
                qs = qi % 2
                qi += 1
                P.dma("sp", lambda e, qs=qs, hq=hq: e.dma_start(out=qh[qs], in_=q_d[hq, :, :]), writes=[qh_b[qs]])
                for (c0, n) in qtiles:
                    ob, sbk = (3, 5) if obi % 2 == 0 else (4, 6)
                    obi += 1
                    attend1(C, qh[qs][:, c0:c0 + n], qh_b[qs], n, chunks_for(ks, False), sA, C.ao[:, hq, c0:c0 + n], C.ao_b, kv_b[ks], ob, sbk)
                if with_ctx:
                    ob, sbk = (3, 5) if obi % 2 == 0 else (4, 6)
                    obi += 1
                    attend1(C, qh[qs][:, TL:T], qh_b[qs], TCX, chunks_for(ks, True), sA, C.ao[:, hq, TL:T], C.ao_b, kv_b[ks], ob, sbk)
        for hb in range(4):
            ks = kvi % 2
            kvi += 1
            load_kv(ks, 2 + hb, 256 + hb * 128)
            qs = qi % 2
            qi += 1
            P.dma("sp", lambda e, qs=qs, hb=hb: e.dma_start(out=qh[qs], in_=q_d[6 + hb, :, :]), writes=[qh_b[qs]])
            for (c0, n) in qtiles:
                attend2(C, qh[qs][:, c0:c0 + n], qh_b[qs], n, chunks_for(ks, False), sB, C.ao[:, 6 + hb, c0:c0 + n], C.ao_b, kv_b[ks])
            if with_ctx:
                attend2(C, qh[qs][:, TL:T], qh_b[qs], TCX, chunks_for(ks, True), sB, C.ao[:, 6 + hb, TL:T], C.ao_b, kv_b[ks])
        P.barrier()
        P.flush()
        ar.reset(mark2)
        qc = [ar.bf16(T) for _ in range(2)]
        qc_b = [Buf() for _ in range(2)]
        kc_ = [ar.bf16(CK) for _ in range(2)]
        vc_ = [ar.bf16(28 * 128).rearrange("p (c d) -> p c d", d=128) for _ in range(2)]
        kvc_b = [Buf() for _ in range(2)]
        tcm = [ar.f32(960) for _ in range(2)]
        tsp = [ar.f32(4096) for _ in range(2)]
        tb_b = [Buf() for _ in range(2)]
        sbt = [ar.f32(512) for _ in range(2)]
        sbt_b = [Buf() for _ in range(2)]
        Ec = [ar.bf16(768) for _ in range(2)]
        Ec_b = [Buf() for _ in range(2)]
        for h in range(6):
            s = h % 2
            P.dma("sp", lambda e, s=s, h=h: e.dma_start(out=qc[s], in_=q_d[10 + h, :, :]), writes=[qc_b[s]])
            P.dma("sp", lambda e, s=s, h=h: e.dma_start(out=kc_[s], in_=kcw_d[h, :, :]), writes=[kvc_b[s]])
            P.dma("sp", lambda e, s=s, h=h: e.dma_start(out=vc_[s][0:64, :, :], in_=vcw_d[:, h * 128:(h + 1) * 128].rearrange("(c p) d -> p c d", p=64)), writes=[kvc_b[s]])
            P.dma("sp", lambda e, s=s, h=h: e.dma_start(out=tcm[s][0:64, :], in_=tcom_d[:, h * 960:(h + 1) * 960]), writes=[tb_b[s]])
            P.dma("sp", lambda e, s=s, h=h: e.dma_start(out=tsp[s][0:64, :], in_=tsp_d[:, h * 4096:(h + 1) * 4096]), writes=[tb_b[s]])

            def qk(rl, s=s):
                st = rl % 2
                q = qc[s][:, rl * 64:(rl + 1) * 64]
                fns = []
                for a in range(8):
                    fns.append(lambda e, a=a: e.matmul(C.ps[0:64, 2 * st, a * 64:(a + 1) * 64], lhsT=kc_[s][:, (rl + a) * 64:(rl + a + 1) * 64], rhs=q, start=True, stop=True))
                for c in range(4):
                    fns.append(lambda e, c=c: e.matmul(C.ps[0:64, 2 * st + 1, c * 64:(c + 1) * 64], lhsT=kc_[s][:, NSLOT * 64 + c * 64:NSLOT * 64 + (c + 1) * 64], rhs=q, start=True, stop=True))
                P.group("pe", fns, reads=[kvc_b[s], qc_b[s]], writes=[C.bank[2 * st], C.bank[2 * st + 1]])
                if rl < 4 or rl >= 12:
                    spi = rl if rl < 4 else rl - 8
                    bias = tsp[s][0:64, spi * 512:(spi + 1) * 512]
                else:
                    bias = tcm[s][0:64, 3 * 64:11 * 64]
                P.op("dve", lambda e: e.scalar_tensor_tensor(out=sbt[st][0:64, :], in0=C.ps[0:64, 2 * st, :], scalar=sA, in1=bias, op0=ALU.mult, op1=ALU.add),
                     reads=[C.bank[2 * st], tb_b[s]], writes=[sbt_b[st]])
                P.op("act", lambda e: e.activation(out=Ec[st][0:64, 0:512], in_=sbt[st][0:64, :], func=AF.Exp), reads=[sbt_b[st]], writes=[Ec_b[st]])
                P.op("act", lambda e: e.activation(out=Ec[st][0:64, 512:768], in_=C.ps[0:64, 2 * st + 1, 0:256], func=AF.Exp, scale=sA), reads=[C.bank[2 * st + 1]], writes=[Ec_b[st]])

            def pvc(rl, s=s, h=h):
                st = rl % 2
                O = C.ps[:, 4 + st, 0:64]
                S = C.ps[:, 6 + st, 0:64]
                fns = []
                for i in range(12):
                    vap = vc_[s][0:64, rl + i, :] if i < 8 else vc_[s][0:64, NSLOT + (i - 8), :]
                    E = Ec[st][0:64, i * 64:(i + 1) * 64]
                    fns.append(lambda e, vap=vap, E=E, i=i: e.matmul(O, lhsT=vap, rhs=E, start=(i == 0), stop=(i == 11)))
                    fns.append(lambda e, E=E, i=i: e.matmul(S, lhsT=C.ones[0:64, :], rhs=E, start=(i == 0), stop=(i == 11)))
                P.group("pe", fns, reads=[Ec_b[st], kvc_b[s], C.cst_b], writes=[C.bank[4 + st], C.bank[6 + st]])
                P.op("dve", lambda e: e.reciprocal(out=C.rc[st][:, 0:64], in_=S), reads=[C.bank[6 + st]], writes=[C.rc_b[st]])
                P.op("dve", lambda e: e.tensor_tensor(out=C.ao[:, 10 + h, rl * 64:(rl + 1) * 64], in0=O, in1=C.rc[st][:, 0:64], op=ALU.mult),
                     reads=[C.bank[4 + st], C.rc_b[st]], writes=[C.ao_b])
            qk(0)
            for rl in range(16):
                if rl + 1 < 16:
                    qk(rl + 1)
                pvc(rl)
            if with_ctx:
                cch = [(kc_[s][:, NSLOT * 64 + c * 64:NSLOT * 64 + (c + 1) * 64], vc_[s][0:64, NSLOT + c, :], 64) for c in range(4)]
                attend1(C, qc[s][:, TL:T], qc_b[s], TCX, cch, sA, C.ao[:, 10 + h, TL:T], C.ao_b, kvc_b[s], 5, 7)
        P.barrier()
        P.flush()
        ar.reset(mark1)
        C.h = ar.bf16(KC * T).rearrange("p (k t) -> p k t", t=T)
        C.h_b = Buf()
        setup_wring(C)
        alloc_norm_bufs(C)
        C.wst = WStream(C, [(wout_d[i, :, :], WT) for i in range(KC)] + ffn_wtiles(wgu_d, wdn_d))
        C.unit = 0
        for kc in range(KC):
            for (c0, n, v) in tiles:
                P.op("act", lambda e, kc=kc, c0=c0, n=n: e.activation(out=C.h[:, kc, c0:c0 + n], in_=C.ao[:, kc, c0:c0 + n], func=AF.Square),
                     reads=[C.ao_b], writes=[C.h_b])
        groups = [(0, 6, 768.0, 0.0)] + [(6 + i, 1, 128.0, math.log(1.0 - lam_init)) for i in range(4)] + [(10, 6, 768.0, 0.0)]
        gcols = [P_AOG + i for i in range(6)] + [P_BOG] * 4 + [P_COG + i for i in range(6)]
        for (k0, nk, dn, lb) in groups:
            outs = mm_unit(C, lambda k: C.ones, lambda k, c0, n, k0=k0: C.h[:, k0 + k, c0:c0 + n], nk, tiles, [C.h_b, C.cst_b])
            for ti, (c0, n, v) in enumerate(tiles):
                o, ob = outs[ti]
                P.op("act", lambda e, o=o, c0=c0, n=n, dn=dn: e.activation(out=C.rstd[:, c0:c0 + n], in_=o, func=AF.Ln, scale=1.0 / dn, bias=EPS),
                     reads=[ob], writes=[C.rstd_b])
                P.op("act", lambda e, c0=c0, n=n, lb=lb: e.activation(out=C.rstd[:, c0:c0 + n], in_=C.rstd[:, c0:c0 + n], func=AF.Exp, scale=-0.5, bias=lb),
                     reads=[C.rstd_b], writes=[C.rstd_b])
            for k in range(k0, k0 + nk):
                for (c0, n, v) in tiles:
                    P.op("dve", lambda e, k=k, c0=c0, n=n: e.scalar_tensor_tensor(out=C.ao[:, k, c0:c0 + n], in0=C.ao[:, k, c0:c0 + n], scalar=C.prm[:, gcols[k]:gcols[k] + 1],
                                                                              in1=C.rstd[:, c0:c0 + n], op0=ALU.mult, op1=ALU.mult),
                         reads=[C.ao_b, C.rstd_b, C.prm_b], writes=[C.ao_b])
        for d in range(KC):
            ws, wb = C.wst.get()
            outs = mm_unit(C, lambda k, ws=ws: ws[:, k * 128:(k + 1) * 128], lambda k, c0, n: C.ao[:, k, c0:c0 + n], KC, tiles, [wb, C.ao_b])
            for ti, (c0, n, v) in enumerate(tiles):
                y, yb = outs[ti]
                P.op("dve", lambda e, y=y, d=d, c0=c0, n=n, v=v: e.scalar_tensor_tensor(out=C.x[:, d, c0:c0 + n], in0=y, scalar=C.ms[:, 5, d, v:v + 1],
                                                                                      in1=C.x[:, d, c0:c0 + n], op0=ALU.mult, op1=ALU.add),
                     reads=[yb, C.ms_b, C.x_b], writes=[C.x_b])
        P.barrier()
        P.flush()
        top = ar.mark()
        ar.reset(mark0)
        alloc_ffn_bufs(C)
        assert ar.mark() <= mark1
        ar.reset(top)
        norm_mod(C, 6, 7, tiles)
        ffn(C, 8, tiles)
        P.dma("sp", lambda e: e.dma_start(out=xo_d.rearrange("p (k t) -> p k t", t=T), in_=C.x), reads=[C.x_b])
        P.flush(final=True)
    return nc


def slot_rows(qd):
    if qd == 0:
        return [4, 5, 6, 7] + list(range(0, 20))
    if qd == 3:
        return list(range(44, 64)) + [56, 57, 58, 59]
    return [16 * qd - 4 + s for s in range(NSLOT)]


def special_bias(tc4, qd):
    rows = slot_rows(qd)
    out = np.empty((64, 6, 8, 8, 64), np.float32)
    for spi in range(8):
        rl = spi if spi < 4 else spi + 8
        r = 16 * qd + rl
        rs = min(max(r - 4, 0), 56)
        win = [rows[rl + a] for a in range(8)]
        assert sorted(win) == list(range(rs, rs + 8)), (qd, rl, win)
        for a in range(8):
            out[:, :, spi, a, :] = tc4[:, :, win[a] - r + 7, :]
    for rl in range(4, 12):
        r = 16 * qd + rl
        assert [rows[rl + a] - r + 7 for a in range(8)] == list(range(3, 11))
    return out.reshape(64, 6 * 8 * 512)


def kernel(_ncores=8, _nlayers=2, _dbg=None, **inp):
    inp = {k: np.asarray(v) for k, v in inp.items()}
    Ws = [prep_layer(inp, l) for l in range(2)]
    mod = run_M(inp, Ws)
    cst = const_tables()
    ropes = [rope_tables(qd) for qd in range(4)]
    cores = list(range(_ncores))
    xs = []
    for i in cores:
        b, qd = i // 4, i % 4
        xt = np.concatenate([inp["x"][b, qd * TL:(qd + 1) * TL], inp["ctx"][b, qd * TCX:(qd + 1) * TCX]], axis=0)
        xs.append(_c(to_fm(xt)).reshape(128, KC * T))
    for l in range(_nlayers):
        W = Ws[l]
        lam_init = 0.8 - 0.6 * math.exp(-0.3 * l)
        in_maps = [{"xT": xs[i], "modc": modc_for(mod, l, i // 4), "prm": W["prm"], "cst": cst, "rope": ropes[i % 4],
                    "wgu": W["ffn1_gu"], "wdn": W["ffn1_dn"], "wqk": W["wqk"], "wv": W["wv"]} for i in cores]
        ra = run_bass_kernel_spmd(get_nc("A", build_A), in_maps, core_ids=cores).results
        if _dbg is not None:
            _dbg["A%d" % l] = ra
        tc4 = W["tc"].reshape(64, 6, 15, 64)
        in_maps = []
        for i in cores:
            b, qd = i // 4, i % 4
            grp = [ra[4 * b + r] for r in range(4)]
            kg = np.concatenate([np.asarray(g["ko"]) for g in grp], axis=0)
            vg = np.concatenate([np.asarray(g["vo"]) for g in grp], axis=0)
            rows = slot_rows(qd)
            kparts = [np.asarray(grp[row // 16]["ko"])[6:12, :, (row % 16) * 64:(row % 16 + 1) * 64] for row in rows]
            kparts += [np.asarray(g["ko"])[6:12, :, TL:T] for g in grp]
            vparts = [np.asarray(grp[row // 16]["vo"])[(row % 16) * 64:(row % 16 + 1) * 64, 768:1536] for row in rows]
            vparts += [np.asarray(g["vo"])[TL:T, 768:1536] for g in grp]
            in_maps.append({"xT": np.asarray(ra[i]["xTo"]), "modc": modc_for(mod, l, b), "prm": W["prm"], "cst": cst,
                            "q": np.asarray(ra[i]["qo"]), "kg": kg, "vg": vg,
                            "kcw": np.ascontiguousarray(np.concatenate(kparts, axis=2)), "vcw": np.ascontiguousarray(np.concatenate(vparts, axis=0)),
                            "tcom": W["tc"], "tsp": special_bias(tc4, qd),
                            "wout": W["wout"], "wgu": W["ffn2_gu"], "wdn": W["ffn2_dn"]})
        with_ctx = l < 1
        rb = run_bass_kernel_spmd(get_nc("B%d" % l, lambda: build_B(with_ctx, lam_init)), in_maps, core_ids=cores).results
        if _dbg is not None:
            _dbg["B%d" % l] = rb
        xs = [np.asarray(r["xTo"]) for r in rb]
    out = np.zeros((2, 4096, D), np.float32)
    for i in cores:
        b, qd = i // 4, i % 4
        out[b, qd * TL:(qd + 1) * TL] = xs[i].reshape(128, KC, T).transpose(2, 1, 0).reshape(T, D)[:TL]
    return out
```
